# Optimizing a Trainium2 kernel written in Bass

```python
import math
import jax, jax.numpy as jnp
from jax import lax
import numpy as np

D_MODEL = 1024
BATCH = 8
SEQ = 4096
DEPTH = 4

N_MIXERS = 2
N_A_LAYERS = (DEPTH + 1) // 2
N_B_LAYERS = DEPTH // 2
N_VRES = max(N_B_LAYERS - 1, 0)

D_FF = 2816
CHUNK = 128
D_SGU = 2 * D_MODEL
SGU_GROUPS = 16
SGU_GROUP_DIM = D_SGU // SGU_GROUPS
RWKV_HEAD = 64
RWKV_HEADS = D_MODEL // RWKV_HEAD
DECAY_LORA = 64
AAA_LORA = 64
MV_LORA = 32
GATE_LORA = 160
GN_EPS = 64e-5
LN_EPS = 1e-5
DN_ALPHA = (2 * DEPTH) ** 0.25
DN_BETA = (8 * DEPTH) ** -0.25

kernel_name = "hybrid_sgu_rwkv7_macaron_deepnorm"


def layer_norm(x, g, b, eps=LN_EPS):
    xf = x.astype(jnp.float32)
    mu = jnp.mean(xf, axis=-1, keepdims=True)
    var = jnp.mean(jnp.square(xf - mu), axis=-1, keepdims=True)
    return ((xf - mu) * lax.rsqrt(var + eps) * g + b).astype(x.dtype)


def swiglu(x, w_in, w_out):
    gate, up = jnp.split(x @ w_in, 2, axis=-1)
    return (jax.nn.silu(gate) * up) @ w_out


def sgu_mixer(x, w_in, b_in, ln_g, ln_b, w_sp, b_sp, w_out):
    bsz, seq, _ = x.shape
    z = jax.nn.gelu(x @ w_in + b_in, approximate=False)
    u, v = jnp.split(z, 2, axis=-1)
    v = layer_norm(v, ln_g, ln_b)
    n_chunks = seq // CHUNK
    v = v.reshape(bsz, n_chunks, CHUNK, SGU_GROUPS, SGU_GROUP_DIM)
    causal = jnp.tril(jnp.ones((CHUNK, CHUNK), dtype=bool))
    w = jnp.where(causal[None], w_sp, 0)
    mixed = jnp.einsum('gts,bcsgd->bctgd', w, v) + b_sp.T[None, None, :, :, None]
    mixed = mixed.reshape(bsz, seq, D_SGU)
    return (u * mixed) @ w_out


def rwkv7_scan(r, decay, k, v, a, b):
    def step(state, inp):
        r_t, w_t, k_t, v_t, a_t, b_t = inp
        sa = jnp.einsum('bhij,bhj->bhi', state, a_t)
        state = (state * w_t[:, :, None, :] + sa[..., None] * b_t[:, :, None, :]
                 + v_t[..., None] * k_t[:, :, None, :])
        y = jnp.einsum('bhij,bhj->bhi', state, r_t)
        return state, y
    s0 = jnp.zeros(r.shape[1:3] + (RWKV_HEAD, RWKV_HEAD), jnp.float32)
    _, y = lax.scan(step, s0, (r, decay, k, v, a, b))
    return y


def rwkv7_mixer(x, v_first, vres, mu, w_rkv, w0, w1, w2, a0, a1, a2,
                g1, g2, k_k, k_a, r_k, gn_g, gn_b, w_o):
    bsz, seq, d = x.shape
    H, N = RWKV_HEADS, RWKV_HEAD
    xx = jnp.pad(x, ((0, 0), (1, 0), (0, 0)))[:, :-1] - x
    x_rkv = x[None] + xx[None] * mu[:3, None, None, :]
    r, k, v = jnp.einsum('nbsd,nde->nbse', x_rkv, w_rkv)
    xw = x + xx * mu[3]
    xa = x + xx * mu[4]
    xg = x + xx * mu[5]
    w = -jax.nn.softplus(-(w0 + jnp.tanh(xw @ w1) @ w2)) - 0.5
    if vres is None:
        v_first = v
    else:
        v0, v1, v2 = vres
        xv = x + xx * mu[2]
        v = v + (v_first - v) * jax.nn.sigmoid(v0 + (xv @ v1) @ v2)
    a = jax.nn.sigmoid(a0 + (xa @ a1) @ a2)
    g = jax.nn.sigmoid(xg @ g1) @ g2
    kk = (k * k_k).reshape(bsz, seq, H, N).astype(jnp.float32)
    kk = kk / jnp.maximum(jnp.linalg.norm(kk, axis=-1, keepdims=True), 1e-12)
    k = k * (1 + (a - 1) * k_a)

    def heads(z):
        return z.reshape(bsz, seq, H, N).astype(jnp.float32)

    def tmaj(z):
        return jnp.moveaxis(z, 1, 0)

    r_h, k_h, v_h, a_h = heads(r), heads(k), heads(v), heads(a)
    decay = jnp.exp(-jnp.exp(heads(w)))
    y = rwkv7_scan(tmaj(r_h), tmaj(decay), tmaj(k_h), tmaj(v_h), tmaj(-kk), tmaj(kk * a_h))
    y = jnp.moveaxis(y, 0, 1)
    mean = jnp.mean(y, axis=-1, keepdims=True)
    var = jnp.mean(jnp.square(y - mean), axis=-1, keepdims=True)
    y = (y - mean) * lax.rsqrt(var + GN_EPS) * gn_g.reshape(H, N) + gn_b.reshape(H, N)
    bonus = jnp.sum(r_h * k_h * r_k, axis=-1, keepdims=True) * v_h
    out = ((y + bonus).reshape(bsz, seq, d).astype(x.dtype) * g) @ w_o
    return out, v_first


def setup_inputs(seed: int = 0) -> dict:
    key = jax.random.key(seed)
    ks = iter(jax.random.split(key, 40))
    f32 = jnp.float32

    def nrm(shape, scale):
        return jax.random.normal(next(ks), shape, f32) * scale

    def unif(shape, lo, hi):
        return jax.random.uniform(next(ks), shape, f32, lo, hi)

    D = D_MODEL
    NA, NB, NV = N_A_LAYERS, N_B_LAYERS, N_VRES
    return {
        "x": nrm((BATCH, SEQ, D), 1.0),
        "ln_g": 1.0 + nrm((DEPTH, 3, D), 0.02),
        "ln_b": nrm((DEPTH, 3, D), 0.02),
        "ffn_w_in": nrm((DEPTH, 2, D, 2 * D_FF), D ** -0.5),
        "ffn_w_out": nrm((DEPTH, 2, D_FF, D), D_FF ** -0.5 * DN_BETA),
        "sgu_w_in": nrm((NA, D, 2 * D_SGU), D ** -0.5),
        "sgu_b_in": nrm((NA, 2 * D_SGU), 0.02),
        "sgu_ln_g": 1.0 + nrm((NA, D_SGU), 0.02),
        "sgu_ln_b": nrm((NA, D_SGU), 0.02),
        "sgu_w_sp": nrm((NA, SGU_GROUPS, CHUNK, CHUNK), CHUNK ** -0.5),
        "sgu_b_sp": 1.0 + nrm((NA, SGU_GROUPS, CHUNK), 0.02),
        "sgu_w_out": nrm((NA, D_SGU, D), D_SGU ** -0.5 * DN_BETA),
        "rwkv_mu": unif((NB, 6, D), 0.0, 1.0),
        "rwkv_w_rkv": nrm((NB, 3, D, D), D ** -0.5),
        "rwkv_w0": unif((NB, D), -6.0, -1.0),
        "rwkv_w1": nrm((NB, D, DECAY_LORA), D ** -0.5),
        "rwkv_w2": nrm((NB, DECAY_LORA, D), 0.1 * DECAY_LORA ** -0.5),
        "rwkv_a0": nrm((NB, D), 0.1),
        "rwkv_a1": nrm((NB, D, AAA_LORA), D ** -0.5),
        "rwkv_a2": nrm((NB, AAA_LORA, D), 0.5 * AAA_LORA ** -0.5),
        "rwkv_v0": nrm((NV, D), 0.1),
        "rwkv_v1": nrm((NV, D, MV_LORA), D ** -0.5),
        "rwkv_v2": nrm((NV, MV_LORA, D), 0.5 * MV_LORA ** -0.5),
        "rwkv_g1": nrm((NB, D, GATE_LORA), D ** -0.5),
        "rwkv_g2": nrm((NB, GATE_LORA, D), GATE_LORA ** -0.5),
        "rwkv_k_k": 0.85 + nrm((NB, D), 0.02),
        "rwkv_k_a": 1.0 + nrm((NB, D), 0.02),
        "rwkv_r_k": nrm((NB, RWKV_HEADS, RWKV_HEAD), 0.1),
        "rwkv_gn_g": 1.0 + nrm((NB, D), 0.02),
        "rwkv_gn_b": nrm((NB, D), 0.02),
        "rwkv_w_o": nrm((NB, D, D), D ** -0.5 * DN_BETA),
    }


def reference(x, ln_g, ln_b, ffn_w_in, ffn_w_out, sgu_w_in, sgu_b_in, sgu_ln_g, sgu_ln_b,
              sgu_w_sp, sgu_b_sp, sgu_w_out, rwkv_mu, rwkv_w_rkv, rwkv_w0, rwkv_w1, rwkv_w2,
              rwkv_a0, rwkv_a1, rwkv_a2, rwkv_v0, rwkv_v1, rwkv_v2, rwkv_g1, rwkv_g2,
              rwkv_k_k, rwkv_k_a, rwkv_r_k, rwkv_gn_g, rwkv_gn_b, rwkv_w_o):
    v_first = None
    for i in range(DEPTH):
        x = layer_norm(DN_ALPHA * x + 0.5 * swiglu(x, ffn_w_in[i, 0], ffn_w_out[i, 0]),
                       ln_g[i, 0], ln_b[i, 0])
        j = i // N_MIXERS
        if i % N_MIXERS == 0:
            mix = sgu_mixer(x, sgu_w_in[j], sgu_b_in[j], sgu_ln_g[j], sgu_ln_b[j],
                            sgu_w_sp[j], sgu_b_sp[j], sgu_w_out[j])
        else:
            vres = None if v_first is None else (rwkv_v0[j - 1], rwkv_v1[j - 1], rwkv_v2[j - 1])
            mix, v_first = rwkv7_mixer(x, v_first, vres, rwkv_mu[j], rwkv_w_rkv[j], rwkv_w0[j],
                                       rwkv_w1[j], rwkv_w2[j], rwkv_a0[j], rwkv_a1[j], rwkv_a2[j],
                                       rwkv_g1[j], rwkv_g2[j], rwkv_k_k[j], rwkv_k_a[j],
                                       rwkv_r_k[j], rwkv_gn_g[j], rwkv_gn_b[j], rwkv_w_o[j])
        x = layer_norm(DN_ALPHA * x + mix, ln_g[i, 1], ln_b[i, 1])
        x = layer_norm(DN_ALPHA * x + 0.5 * swiglu(x, ffn_w_in[i, 1], ffn_w_out[i, 1]),
                       ln_g[i, 2], ln_b[i, 2])
    return x
```

```python
import math
from contextlib import ExitStack
import numpy as np
import concourse.bass as bass
import concourse.mybir as mybir
from concourse.bass_utils import run_bass_kernel_spmd

F32 = mybir.dt.float32
BF16 = mybir.dt.bfloat16
AF = mybir.ActivationFunctionType
ALU = mybir.AluOpType

D = 1024
SEQ = 4096
DEPTH = 4
DFF = 2816
NF = DFF // 128
DSGU = 2048
TILE = 512
ALPHA = (2 * DEPTH) ** 0.25
LN_EPS = 1e-5
GN_EPS = 64e-5
CDEC = math.exp(-0.5)

WNAMES = ["ffn_w_in", "ffn_w_out", "sgu_w_in", "sgu_w_out", "rwkv_w_rkv", "rwkv_w1", "rwkv_w2",
          "rwkv_a1", "rwkv_a2", "rwkv_v1", "rwkv_v2", "rwkv_g1", "rwkv_g2", "rwkv_w_o"]
SHAPES = {
    "ln_g": [4, 3, 1024], "ln_b": [4, 3, 1024], "ffn_w_in": [4, 2, 1024, 5632],
    "ffn_w_out": [4, 2, 2816, 1024], "sgu_w_in": [2, 1024, 4096], "sgu_b_in": [2, 4096],
    "sgu_ln_g": [2, 2048], "sgu_ln_b": [2, 2048], "sgu_w_sp": [2, 16, 128, 128],
    "sgu_b_sp": [2, 16, 128], "sgu_w_out": [2, 2048, 1024], "rwkv_mu": [2, 6, 1024],
    "rwkv_w_rkv": [2, 3, 1024, 1024], "rwkv_w0": [2, 1024], "rwkv_w1": [2, 1024, 64],
    "rwkv_w2": [2, 64, 1024], "rwkv_a0": [2, 1024], "rwkv_a1": [2, 1024, 64],
    "rwkv_a2": [2, 64, 1024], "rwkv_v0": [1, 1024], "rwkv_v1": [1, 1024, 32],
    "rwkv_v2": [1, 32, 1024], "rwkv_g1": [2, 1024, 160], "rwkv_g2": [2, 160, 1024],
    "rwkv_k_k": [2, 1024], "rwkv_k_a": [2, 1024], "rwkv_r_k": [2, 16, 64],
    "rwkv_gn_g": [2, 1024], "rwkv_gn_b": [2, 1024], "rwkv_w_o": [2, 1024, 1024],
}


class SemC:
    _n = 0

    def __init__(self, nc, name):
        self.h = nc.alloc_semaphore(name)
        self.count = 0
        SemC._n += 1
        self.id = SemC._n


class Buf:
    def __init__(self, ap, sem=None):
        self.ap = ap
        self.w = {}
        self.r = {}
        self.sem = sem


class Eng:
    def __init__(self, nc, name, e):
        self.name = name
        self.e = e
        self.sem = SemC(nc, "s_" + name)
        self.waited = {}


def _merge(dst, src):
    for k, (s, v) in src.items():
        if k not in dst or dst[k][1] < v:
            dst[k] = (s, v)


class KB:
    def __init__(self, n_tiles, n_sub, dbg=False):
        self.n_tiles = n_tiles
        self.n_sub = n_sub
        nc = self.nc = bass.Bass("TRN2", target_bir_lowering=False)
        self.pe = Eng(nc, "pe", nc.tensor)
        self.act = Eng(nc, "act", nc.scalar)
        self.dve = Eng(nc, "dve", nc.vector)
        self.pool = Eng(nc, "pool", nc.gpsimd)
        self.sp = Eng(nc, "sp", nc.sync)
        self.comp = [self.pe, self.act, self.dve, self.pool]
        self.uid = 0
        self.bank_i = 0
        self.ev_i = 0
        T = n_tiles * TILE
        self.x = nc.dram_tensor("x", [T, D], F32, kind="ExternalInput").ap()
        self.y = nc.dram_tensor("y", [T, D], F32, kind="ExternalOutput").ap()
        self.inp = {}
        for k, shp in SHAPES.items():
            self.inp[k] = nc.dram_tensor(k, shp, F32, kind="ExternalInput").ap()
        self.scr = {}
        for k in WNAMES:
            self.scr[k] = nc.dram_tensor("sc_" + k, SHAPES[k], BF16, kind="Internal").ap()

    def name(self, p):
        self.uid += 1
        return f"{p}_{self.uid}"

    def sb(self, shape, dt=F32, sem=None, ctx=None, nm="t"):
        if ctx is None:
            t = self.nc.alloc_sbuf_tensor(self.name(nm), list(shape), dt)
        else:
            t = ctx.enter_context(self.nc.sbuf_tensor(self.name(nm), list(shape), dt))
        return Buf(t.ap(), sem)

    def dsb(self, shape, dt=F32, ctx=None, nm="d"):
        return self.sb(shape, dt, SemC(self.nc, self.name("sd")), ctx, nm)

    def _wait(self, eng, deps):
        for sid, (s, v) in deps.items():
            if eng is self.pe and s is self.pe.sem:
                continue
            if eng.waited.get(sid, 0) < v:
                eng.e.wait_ge(s.h, v)
                eng.waited[sid] = v

    def op(self, eng, fn, reads=(), writes=(), inc=True):
        deps = {}
        for b in reads:
            _merge(deps, b.w)
        for b in writes:
            _merge(deps, b.w)
            _merge(deps, b.r)
        self._wait(eng, deps)
        ins = fn(eng.e)
        s = eng.sem
        if inc:
            s.count += 1
            ins.then_inc(s.h, 1)
            tok = {s.id: (s, s.count)}
        else:
            tok = {s.id: (s, s.count + 1)}
        for b in reads:
            _merge(b.r, tok)
        for b in writes:
            _merge(b.w, tok)
        return ins

    def dma(self, q, out_ap, in_ap, sem, reads=(), writes=()):
        deps = {}
        for b in reads:
            _merge(deps, b.w)
        for b in writes:
            _merge(deps, b.w)
            _merge(deps, b.r)
        self._wait(q, deps)
        ins = q.e.dma_start(out=out_ap, in_=in_ap)
        sem.count += 16
        ins.then_inc(sem.h, 16)
        tok = {sem.id: (sem, sem.count)}
        for b in reads:
            _merge(b.r, tok)
        for b in writes:
            _merge(b.w, tok)

    def barrier(self):
        for e in self.comp:
            deps = {}
            for o in self.comp:
                if o is not e and o.sem.count > 0:
                    deps[o.sem.id] = (o.sem, o.sem.count)
            self._wait(e, deps)

    def bank(self):
        b = self.banks[self.bank_i % 8]
        self.bank_i += 1
        return b

    def evq(self):
        self.ev_i += 1
        return self.act if self.ev_i % 2 == 0 else self.dve

    def copy(self, eng, out, in_, reads, writes):
        if eng is self.act:
            return self.op(eng, lambda e: e.activation(out=out, in_=in_, func=AF.Copy), reads, writes)
        return self.op(eng, lambda e: e.tensor_copy(out=out, in_=in_), reads, writes)

    def mm(self, out, lhsT, rhs, start, stop, reads, writes, inc=None):
        return self.op(self.pe, lambda e: e.matmul(out, lhsT, rhs, start=start, stop=stop),
                       reads, writes, inc=(stop if inc is None else inc))

    def setup(self):
        nc = self.nc
        self.banks = [Buf(nc.alloc_psum_tensor(f"ps{i}", [128, 512], F32).ap()) for i in range(8)]
        P = self.pool
        self.ident32 = self.sb([128, 128])
        self.identb = self.sb([128, 128], BF16)
        self.bones = self.sb([128, 128], BF16)
        self.mask4 = self.sb([128, 2, 2, 128])
        self.maskL = self.sb([128, 128])
        self.sgmask = self.sb([128, 128])
        self.rmask = self.sb([128, 1024])
        tmp = self.sb([128, 128])

        def tri(bf, base, cm, step):
            buf_ap = bf.ap[:]
            self.op(P, lambda e: e.memset(buf_ap, 1.0), (), (bf,))
            self.op(P, lambda e: e.affine_select(out=buf_ap, in_=buf_ap, pattern=[[step, 128]],
                                                 compare_op=ALU.is_ge, fill=0.0, base=base,
                                                 channel_multiplier=cm), (), (bf,))
        self.op(P, lambda e: e.memset(self.ident32.ap[:], 1.0), (), (self.ident32,))
        self.op(P, lambda e: e.affine_select(out=self.ident32.ap[:], in_=self.ident32.ap[:],
                                             pattern=[[1, 128]], compare_op=ALU.is_equal, fill=0.0,
                                             base=0, channel_multiplier=-1), (), (self.ident32,))
        self.op(P, lambda e: e.tensor_copy(out=self.identb.ap[:], in_=self.ident32.ap[:]),
                (self.ident32,), (self.identb,))
        self.op(P, lambda e: e.memset(self.bones.ap[:], 0.0), (), (self.bones,))
        self.op(P, lambda e: e.memset(self.bones.ap[0:64, 0:64], 1.0), (), (self.bones,))
        self.op(P, lambda e: e.memset(self.bones.ap[64:128, 64:128], 1.0), (), (self.bones,))
        tri(self.sgmask, 0, -1, 1)
        tri(tmp, 0, -1, 1)
        self.op(P, lambda e: e.memset(tmp.ap[0:64, 64:128], 0.0), (), (tmp,))
        for w in range(2):
            self.op(P, lambda e: e.tensor_copy(out=self.mask4.ap[:, w, 1, :], in_=tmp.ap[:]), (tmp,), (self.mask4,))
        tri(tmp, -1, -1, 1)
        self.op(P, lambda e: e.memset(tmp.ap[0:64, 64:128], 0.0), (), (tmp,))
        for w in range(2):
            self.op(P, lambda e: e.tensor_copy(out=self.mask4.ap[:, w, 0, :], in_=tmp.ap[:]), (tmp,), (self.mask4,))
        tri(self.maskL, -1, 1, -1)
        self.op(P, lambda e: e.memset(self.maskL.ap[64:128, 0:64], 0.0), (), (self.maskL,))
        self.op(P, lambda e: e.memset(self.rmask.ap[:], 1.0), (), (self.rmask,))
        rv = self.rmask.ap.rearrange("p (c t) -> p c t", t=64)
        self.op(P, lambda e: e.memset(rv[:, :, 0:1], 0.0), (), (self.rmask,))

        self.x_tm = [self.dsb([128, D], nm="xtm") for _ in range(4)]
        self.st_sem = [SemC(nc, f"st{i}") for i in range(4)]
        self.ring = [self.dsb([128, 8, 512], BF16, nm="ring") for _ in range(2)]
        self.ring_i = 0
        self.lng = self.dsb([128, D], nm="lng")
        self.lnb = self.dsb([128, D], nm="lnb")
        self.pf = self.sb([128, 8, 36])
        self.wspT = [self.sb([128, 16, 128], BF16) for _ in range(2)]
        self.S32 = [[self.sb([128, 128]) for _ in range(8)] for _ in range(2)]
        self.Sb = [[self.sb([128, 128], BF16) for _ in range(8)] for _ in range(2)]
        self.xlast = [self.sb([128, 8]) for _ in range(2)]
        self.vfirst = self.sb([128, 4, 8, 128])
        self.lora1 = self.dsb([128, 8, 320], BF16, nm="lora1")
        self.lora2 = [self.dsb([p, D], BF16, nm="lora2") for p in (64, 64, 32, 128, 32)]
        self.epsln = self.sb([128, 4])
        self.op(P, lambda e: e.memset(self.epsln.ap[:, 0:1], LN_EPS / (ALPHA * ALPHA)), (), (self.epsln,))
        self.op(P, lambda e: e.memset(self.epsln.ap[:, 1:2], LN_EPS), (), (self.epsln,))
        self.op(P, lambda e: e.memset(self.epsln.ap[:, 2:3], 1e-24), (), (self.epsln,))
        self.op(P, lambda e: e.memset(self.epsln.ap[:, 3:4], GN_EPS), (), (self.epsln,))
        for j in range(2):
            for hp in range(8):
                self.op(P, lambda e: e.memset(self.S32[j][hp].ap[:], 0.0), (), (self.S32[j][hp],))
                self.op(P, lambda e: e.memset(self.Sb[j][hp].ap[:], 0.0), (), (self.Sb[j][hp],))
            self.op(P, lambda e: e.memset(self.xlast[j].ap[:], 0.0), (), (self.xlast[j],))

    def setup_params(self):
        with ExitStack() as ctx:
            rows = self.dsb([128, D], ctx=ctx, nm="prow")
            self.op(self.pool, lambda e: e.memset(rows.ap[:], 0.0), (), (rows,))
            I = self.inp
            srcs = []
            for j in range(2):
                for m in range(6):
                    srcs.append(I["rwkv_mu"][j, m:m + 1, :])
                srcs += [I["rwkv_w0"][j:j + 1, :], I["rwkv_a0"][j:j + 1, :], I["rwkv_k_k"][j:j + 1, :],
                         I["rwkv_k_a"][j:j + 1, :],
                         I["rwkv_r_k"][j:j + 1].rearrange("o h n -> o (h n)"),
                         I["rwkv_gn_g"][j:j + 1, :], I["rwkv_gn_b"][j:j + 1, :],
                         I["rwkv_v0"][0:1, :]]
            for j in range(2):
                for h in range(2):
                    srcs.append(I["sgu_b_in"][j:j + 1, h * 1024:(h + 1) * 1024])
            assert len(srcs) == 32
            for i, s in enumerate(srcs):
                self.dma(self.sp, rows.ap[i:i + 1, :], s, rows.sem, (), (rows,))
            for kc in range(8):
                b = self.bank()
                self.op(self.pe, lambda e: e.transpose(b.ap[:, 0:128], rows.ap[:, kc * 128:(kc + 1) * 128],
                                                       self.ident32.ap[:]),
                        (rows, self.ident32), (b,))
                self.copy(self.evq(), self.pf.ap[:, kc, 0:32], b.ap[:, 0:32], (b,), (self.pf,))
            for j in range(2):
                self.op(self.dve, lambda e: e.tensor_scalar(out=self.pf.ap[:, :, 32 + j:33 + j],
                                                            in0=self.pf.ap[:, :, 14 * j + 9:14 * j + 10],
                                                            scalar1=-1.0, scalar2=1.0, op0=ALU.mult, op1=ALU.add),
                        (self.pf,), (self.pf,))
            for j in range(2):
                wn = self.dsb([128, 16, 128], ctx=ctx, nm="wspn")
                self.dma(self.sp, wn.ap[:], self.inp["sgu_w_sp"][j].rearrange("g t s -> t g s"), wn.sem, (), (wn,))
                for g4 in range(4):
                    b = self.bank()
                    for gi in range(4):
                        g = g4 * 4 + gi
                        self.op(self.pe, lambda e: e.transpose(b.ap[:, gi * 128:(gi + 1) * 128], wn.ap[:, g, :],
                                                               self.ident32.ap[:]),
                                (wn, self.ident32), (b,), inc=(gi == 3))
                    for gi in range(4):
                        g = g4 * 4 + gi
                        self.op(self.dve, lambda e: e.tensor_tensor(out=self.wspT[j].ap[:, g, :],
                                                                    in0=b.ap[:, gi * 128:(gi + 1) * 128],
                                                                    in1=self.sgmask.ap[:], op=ALU.mult),
                                (b, self.sgmask), (self.wspT[j],))
            self.barrier()

    def prepass(self):
        engs = [self.act, self.dve, self.pool]
        with ExitStack() as ctx:
            stg = [self.dsb([128, 5632], ctx=ctx, nm="stg") for _ in range(2)]
            stb = [self.sb([128, 5632], BF16, SemC(self.nc, self.name("so")), ctx, "stb") for _ in range(2)]
            self.pre_sem = {}
            i = 0
            for k in WNAMES:
                src = self.inp[k]
                dst = self.scr[k]
                psem = self.pre_sem[k] = SemC(self.nc, "pre_" + k)
                if k == "ffn_w_in":
                    s2 = src.rearrange("l j (kc p) c -> (l j kc) p c", p=128)
                    d2 = dst.rearrange("l j (kc p) c -> (l j kc) p c", p=128)
                    chunks = [(s2[q], d2[q], 5632, True) for q in range(64)]
                else:
                    n = 1
                    for v in SHAPES[k]:
                        n *= v
                    per = n // 128
                    names = " ".join(f"a{q}" for q in range(len(SHAPES[k])))
                    sf = src.rearrange(f"{names} -> ({names})").rearrange("(p x) -> p x", p=128)
                    df = dst.rearrange(f"{names} -> ({names})").rearrange("(p x) -> p x", p=128)
                    chunks = []
                    x0 = 0
                    while x0 < per:
                        w = min(4096, per - x0)
                        chunks.append((sf[:, x0:x0 + w], df[:, x0:x0 + w], w, False))
                        x0 += w
                for (s_ap, d_ap, w, perm) in chunks:
                    a = stg[i % 2]
                    b = stb[i % 2]
                    self.dma(self.sp, a.ap[:, 0:w], s_ap, a.sem, (), (a,))
                    eng = engs[i % 3]
                    if perm:
                        o_ap = b.ap.rearrange("p (f g n) -> p g f n", g=2, n=128)
                        i_ap = a.ap.rearrange("p (g f n) -> p g f n", g=2, n=128)
                    else:
                        o_ap = b.ap[:, 0:w]
                        i_ap = a.ap[:, 0:w]
                    self.copy(eng, o_ap, i_ap, (a,), (b,))
                    deps = {}
                    _merge(deps, b.w)
                    self._wait(self.sp, deps)
                    ins = self.sp.e.dma_start(out=d_ap, in_=b.ap[:, 0:w])
                    b.sem.count += 16
                    ins.then_inc(b.sem.h, 16)
                    _merge(b.r, {b.sem.id: (b.sem, b.sem.count)})
                    i += 1
            for b in stb:
                for e in self.comp + [self.sp]:
                    self._wait(e, dict(b.r))
            self.barrier()

    def slot(self):
        s = self.ring[self.ring_i % len(self.ring)]
        self.ring_i += 1
        return s

    def load_kc_block(self, mat_ap, c0, width):
        s = self.slot()
        src = mat_ap[:, c0:c0 + width].rearrange("(kc p) c -> p kc c", p=128)
        self.dma(self.sp, s.ap[:, :, 0:width], src, s.sem, (), (s,))
        return s

    def load_rows_block(self, mat_ap, r0, nchunks):
        s = self.slot()
        v = s.ap.rearrange("p a b -> p (a b)").rearrange("p (c n) -> p c n", n=1024)
        src = mat_ap[r0:r0 + nchunks * 128, :].rearrange("(c p) n -> p c n", p=128)
        self.dma(self.sp, v[:, 0:nchunks, :], src, s.sem, (), (s,))
        return s, v

    def transpose_x(self, xT, dt_out):
        for kc in range(8):
            b = self.bank()
            for tc in range(4):
                self.op(self.pe, lambda e: e.transpose(b.ap[:, tc * 128:(tc + 1) * 128],
                                                       self.x_tm[tc].ap[:, kc * 128:(kc + 1) * 128],
                                                       self.ident32.ap[:]),
                        (self.x_tm[tc], self.ident32), (b,), inc=(tc == 3))
            self.copy(self.evq(), xT[kc].ap[:, 0:512], b.ap[:], (b,), (xT[kc],))

    def load_ln(self, l, j):
        self.dma(self.act, self.lng.ap[:], self.inp["ln_g"][l, j, :].partition_broadcast(128), self.lng.sem, (), (self.lng,))
        self.dma(self.act, self.lnb.ap[:], self.inp["ln_b"][l, j, :].partition_broadcast(128), self.lnb.sem, (), (self.lnb,))

    def resid_ln(self, tc, b0, b1, coef, ctx_bufs):
        ytmp, stats, mv, sc = ctx_bufs
        xt = self.x_tm[tc]
        for dh, b in enumerate((b0, b1)):
            self.op(self.dve, lambda e: e.scalar_tensor_tensor(out=ytmp.ap[:, dh * 512:(dh + 1) * 512], in0=b.ap[:],
                                                               scalar=coef, in1=xt.ap[:, dh * 512:(dh + 1) * 512],
                                                               op0=ALU.mult, op1=ALU.add),
                    (b, xt), (ytmp,))
            self.op(self.dve, lambda e: e.bn_stats(out=stats.ap[:, dh * 6:(dh + 1) * 6], in_=ytmp.ap[:, dh * 512:(dh + 1) * 512]),
                    (ytmp,), (stats,))
        self.op(self.dve, lambda e: e.bn_aggr(out=mv.ap[:, 0:2], in_=stats.ap[:]), (stats,), (mv,))
        self.op(self.act, lambda e: e.activation(out=sc.ap[:, 0:1], in_=mv.ap[:, 1:2], func=AF.Sqrt,
                                                 bias=self.epsln.ap[:, 0:1], scale=1.0), (mv, self.epsln), (sc,))
        self.op(self.dve, lambda e: e.reciprocal(out=sc.ap[:, 1:2], in_=sc.ap[:, 0:1]), (sc,), (sc,))
        self.op(self.dve, lambda e: e.scalar_tensor_tensor(out=sc.ap[:, 2:3], in0=mv.ap[:, 0:1], scalar=-1.0,
                                                           in1=sc.ap[:, 1:2], op0=ALU.mult, op1=ALU.mult),
                (mv, sc), (sc,))
        self.op(self.act, lambda e: e.activation(out=ytmp.ap[:], in_=ytmp.ap[:], func=AF.Identity,
                                                 bias=sc.ap[:, 2:3], scale=sc.ap[:, 1:2]), (ytmp, sc), (ytmp,))
        self.op(self.pool, lambda e: e.tensor_tensor(out=ytmp.ap[:], in0=ytmp.ap[:], in1=self.lng.ap[:], op=ALU.mult),
                (ytmp, self.lng), (ytmp,))
        self.op(self.pool, lambda e: e.tensor_tensor(out=xt.ap[:], in0=ytmp.ap[:], in1=self.lnb.ap[:], op=ALU.add),
                (ytmp, self.lnb), (xt,))

    def ln_bufs(self, ctx):
        return (self.sb([128, D], ctx=ctx), self.sb([128, 12], ctx=ctx), self.sb([128, 2], ctx=ctx),
                self.sb([128, 4], ctx=ctx))

    def ffn(self, l, j):
        win = self.scr["ffn_w_in"][l, j]
        wout = self.scr["ffn_w_out"][l, j]
        self.barrier()
        with ExitStack() as ctx:
            xT = [self.sb([128, 512], BF16, ctx=ctx) for _ in range(8)]
            gT = [self.sb([128, 512], BF16, ctx=ctx) for _ in range(NF)]
            sg = [self.sb([128, 512], ctx=ctx) for _ in range(2)]
            lnb = self.ln_bufs(ctx)
            import os
            fst = int(os.environ.get("FSTAGE", "9"))
            if fst >= 5:
                self.load_ln(l, 0 if j == 0 else 2)
            self.transpose_x(xT, BF16)
            for fb in range(NF // 2 if fst >= 2 else 0):
                s = self.load_kc_block(win, fb * 512, 512)
                for fi in range(2):
                    f = fb * 2 + fi
                    bg, bu = self.bank(), self.bank()
                    for g, b in enumerate((bg, bu)):
                        for kc in range(8):
                            c0 = fi * 256 + g * 128
                            self.mm(b.ap[:], s.ap[:, kc, c0:c0 + 128], xT[kc].ap[:], kc == 0, kc == 7,
                                    (s, xT[kc]), (b,))
                    t = sg[f % 2]
                    self.op(self.act, lambda e: e.activation(out=t.ap[:], in_=bg.ap[:], func=AF.Silu), (bg,), (t,))
                    self.op(self.dve, lambda e: e.tensor_tensor(out=gT[f].ap[:], in0=t.ap[:], in1=bu.ap[:], op=ALU.mult),
                            (t, bu), (gT[f],))
            bk = [[self.bank(), self.bank()] for _ in range(4)]
            for fq in range(6 if fst >= 3 else 0):
                nch = min(4, NF - fq * 4)
                s, v = self.load_rows_block(wout, fq * 512, nch)
                for fi in range(nch):
                    f = fq * 4 + fi
                    for tc in range(4):
                        for dh in range(2):
                            self.mm(bk[tc][dh].ap[:], gT[f].ap[:, tc * 128:(tc + 1) * 128],
                                    v[:, fi, dh * 512:(dh + 1) * 512], f == 0, f == NF - 1,
                                    (gT[f], s), (bk[tc][dh],), inc=True)
            for tc in range(4 if fst >= 5 else 0):
                self.resid_ln(tc, bk[tc][0], bk[tc][1], 0.5 / ALPHA, lnb)

    def sgu(self, l, j):
        win = self.scr["sgu_w_in"][j]
        wout = self.scr["sgu_w_out"][j]
        I = self.inp
        self.barrier()
        with ExitStack() as ctx:
            xT = [self.sb([128, 512], BF16, ctx=ctx) for _ in range(8)]
            uT = [self.sb([128, 512], BF16, ctx=ctx) for _ in range(16)]
            vtm = [self.sb([128, DSGU], BF16, ctx=ctx) for _ in range(4)]
            vfl = [self.sb([128, DSGU], ctx=ctx) for _ in range(1)]
            tmp = [self.sb([128, 512], ctx=ctx) for _ in range(2)]
            bc = [self.sb([128, DSGU], sem=self.sgu_sems[i], ctx=ctx) for i in range(4)]
            st = self.sb([128, 24], ctx=ctx)
            mv = self.sb([128, 2], ctx=ctx)
            sc = self.sb([128, 4], ctx=ctx)
            lnb = self.ln_bufs(ctx)
            self.load_ln(l, 1)
            self.dma(self.act, bc[0].ap[:], I["sgu_b_in"][j, 2048:4096].partition_broadcast(128), bc[0].sem, (), (bc[0],))
            self.dma(self.act, bc[1].ap[:], I["sgu_ln_g"][j, :].partition_broadcast(128), bc[1].sem, (), (bc[1],))
            self.dma(self.act, bc[2].ap[:], I["sgu_ln_b"][j, :].partition_broadcast(128), bc[2].sem, (), (bc[2],))
            self.dma(self.act, bc[3].ap[:], I["sgu_b_sp"][j].rearrange("g t -> (g t)").partition_broadcast(128),
                     bc[3].sem, (), (bc[3],))
            self.transpose_x(xT, BF16)
            import os
            gst = int(os.environ.get("GSTAGE", "9"))
            for cb in range(4 if gst >= 1 else 0):
                s = self.load_kc_block(win, cb * 512, 512)
                for ci in range(4):
                    c = cb * 4 + ci
                    b = self.bank()
                    for kc in range(8):
                        self.mm(b.ap[:], s.ap[:, kc, ci * 128:(ci + 1) * 128], xT[kc].ap[:], kc == 0, kc == 7,
                                (s, xT[kc]), (b,))
                    bias = self.pf.ap[:, c % 8, 28 + 2 * j + c // 8:29 + 2 * j + c // 8]
                    self.op(self.act, lambda e: e.activation(out=uT[c].ap[:], in_=b.ap[:], func=AF.Gelu, bias=bias,
                                                             scale=1.0), (b, self.pf), (uT[c],))
            for tc in range(4 if gst >= 2 else 0):
                vf = vfl[0]
                for ng in range(4):
                    s = self.load_kc_block(win, 2048 + ng * 512, 512)
                    b = self.bank()
                    for kc in range(8):
                        self.mm(b.ap[:], xT[kc].ap[:, tc * 128:(tc + 1) * 128], s.ap[:, kc, :], kc == 0, kc == 7,
                                (s, xT[kc]), (b,))
                    t = tmp[ng % 2]
                    self.op(self.dve, lambda e: e.tensor_tensor(out=t.ap[:], in0=b.ap[:],
                                                                in1=bc[0].ap[:, ng * 512:(ng + 1) * 512], op=ALU.add),
                            (b, bc[0]), (t,))
                    self.op(self.act, lambda e: e.activation(out=vf.ap[:, ng * 512:(ng + 1) * 512],
                                                             in_=t.ap[:], func=AF.Gelu), (t,), (vf,))
                for q in range(4):
                    self.op(self.dve, lambda e: e.bn_stats(out=st.ap[:, q * 6:(q + 1) * 6], in_=vf.ap[:, q * 512:(q + 1) * 512]),
                            (vf,), (st,))
                self.op(self.dve, lambda e: e.bn_aggr(out=mv.ap[:, 0:2], in_=st.ap[:]), (st,), (mv,))
                self.op(self.act, lambda e: e.activation(out=sc.ap[:, 0:1], in_=mv.ap[:, 1:2], func=AF.Sqrt,
                                                         bias=self.epsln.ap[:, 1:2], scale=1.0), (mv, self.epsln), (sc,))
                self.op(self.dve, lambda e: e.reciprocal(out=sc.ap[:, 1:2], in_=sc.ap[:, 0:1]), (sc,), (sc,))
                self.op(self.dve, lambda e: e.scalar_tensor_tensor(out=sc.ap[:, 2:3], in0=mv.ap[:, 0:1], scalar=-1.0,
                                                                   in1=sc.ap[:, 1:2], op0=ALU.mult, op1=ALU.mult),
                        (mv, sc), (sc,))
                self.op(self.act, lambda e: e.activation(out=vf.ap[:], in_=vf.ap[:], func=AF.Identity,
                                                         bias=sc.ap[:, 2:3], scale=sc.ap[:, 1:2]), (vf, sc), (vf,))
                self.op(self.pool, lambda e: e.tensor_tensor(out=vf.ap[:], in0=vf.ap[:], in1=bc[1].ap[:], op=ALU.mult),
                        (vf, bc[1]), (vf,))
                self.op(self.pool, lambda e: e.tensor_tensor(out=vtm[tc].ap[:], in0=vf.ap[:], in1=bc[2].ap[:], op=ALU.add),
                        (vf, bc[2]), (vtm[tc],))
            for g in range(16 if gst >= 3 else 0):
                b = self.bank()
                for tc in range(4):
                    self.mm(b.ap[:, tc * 128:(tc + 1) * 128], vtm[tc].ap[:, g * 128:(g + 1) * 128],
                            self.wspT[j].ap[:, g, :], True, True, (vtm[tc], self.wspT[j]), (b,))
                t = tmp[g % 2]
                for tc in range(4):
                    self.op(self.dve, lambda e: e.tensor_tensor(out=t.ap[:, tc * 128:(tc + 1) * 128],
                                                                in0=b.ap[:, tc * 128:(tc + 1) * 128],
                                                                in1=bc[3].ap[:, g * 128:(g + 1) * 128], op=ALU.add),
                            (b, bc[3]), (t,))
                self.op(self.pool, lambda e: e.tensor_tensor(out=uT[g].ap[:], in0=t.ap[:], in1=uT[g].ap[:], op=ALU.mult),
                        (t, uT[g]), (uT[g],))
            bk = [[self.bank(), self.bank()] for _ in range(4)]
            for fq in range(4 if gst >= 4 else 0):
                s, v = self.load_rows_block(wout, fq * 512, 4)
                for fi in range(4):
                    f = fq * 4 + fi
                    for tc in range(4):
                        for dh in range(2):
                            self.mm(bk[tc][dh].ap[:], uT[f].ap[:, tc * 128:(tc + 1) * 128],
                                    v[:, fi, dh * 512:(dh + 1) * 512], f == 0, f == 15, (uT[f], s), (bk[tc][dh],),
                                    inc=True)
            for tc in range(4 if gst >= 4 else 0):
                self.resid_ln(tc, bk[tc][0], bk[tc][1], 1.0 / ALPHA, lnb)

    def build(self):
        self.setup()
        self.sgu_sems = [SemC(self.nc, f"sgub{i}") for i in range(4)]
        import os
        stg = int(os.environ.get("KSTAGE", "9"))
        if stg >= 2:
            self.prepass()
        if stg >= 3:
            self.setup_params()
        sub = 0
        for t in range(self.n_tiles):
            for tc in range(4):
                r0 = t * TILE + tc * 128
                self.dma(self.sp, self.x_tm[tc].ap[:], self.x[r0:r0 + 128, :], self.x_tm[tc].sem, (), (self.x_tm[tc],))
            done = 0
            for l in range(DEPTH):
                for step in range(3):
                    if done >= self.n_sub:
                        break
                    if step == 0:
                        self.ffn(l, 0)
                    elif step == 2:
                        self.ffn(l, 1)
                    elif l % 2 == 0:
                        self.sgu(l, l // 2)
                    else:
                        self.rwkv(l, l // 2, t)
                    done += 1
            for tc in range(4):
                r0 = t * TILE + tc * 128
                self.dma(self.pool, self.y[r0:r0 + 128, :], self.x_tm[tc].ap[:], self.st_sem[tc], (self.x_tm[tc],), ())
        fin = {}
        for s in self.st_sem:
            fin[s.id] = (s, s.count)
        self._wait(self.pool, fin)
        return self.nc

    def rwkv(self, l, j, t):
        I = self.inp
        S = self.scr
        pf = self.pf
        P0 = 14 * j
        DV, AC, PL, PE = self.dve, self.act, self.pool, self.pe

        def pc(hp, idx):
            return pf.ap[:, hp, idx:idx + 1]
        self.barrier()
        with ExitStack() as ctx:
            def f32(shape=(128, 4, 128)):
                return self.sb(list(shape), F32, ctx=ctx)

            def b16(shape=(128, 4, 128)):
                return self.sb(list(shape), BF16, ctx=ctx)
            lnb = self.ln_bufs(ctx)
            self.load_ln(l, 1)
            L1 = self.lora1
            for (nm, c0, wd, jj) in (("rwkv_w1", 0, 64, j), ("rwkv_a1", 64, 64, j), ("rwkv_v1", 128, 32, 0),
                                     ("rwkv_g1", 160, 160, j)):
                self.dma(self.sp, L1.ap[:, :, c0:c0 + wd], S[nm][jj].rearrange("(kc p) n -> p kc n", p=128), L1.sem,
                         (), (L1,))
            L2 = self.lora2
            self.dma(self.sp, L2[0].ap[:], S["rwkv_w2"][j], L2[0].sem, (), (L2[0],))
            self.dma(self.sp, L2[1].ap[:], S["rwkv_a2"][j], L2[1].sem, (), (L2[1],))
            self.dma(self.sp, L2[2].ap[:], S["rwkv_v2"][0], L2[2].sem, (), (L2[2],))
            self.dma(self.sp, L2[3].ap[:], S["rwkv_g2"][j, 0:128, :], L2[3].sem, (), (L2[3],))
            self.dma(self.sp, L2[4].ap[:], S["rwkv_g2"][j, 128:160, :], L2[4].sem, (), (L2[4],))
            xT32 = f32((128, 8, 129))
            xx = f32((128, 8, 128))
            xm = [b16((128, 8, 128)) for _ in range(6)]
            hl = b16((128, 5, 128))
            yg = b16((128, 8, 128))
            r32, k32, v32, lds, a32, g32, Lc, G, Gi, Gp, T1, bon, y32 = [f32() for _ in range(13)]
            sqb, rkb, vb, yb = b16(), b16(), b16(), b16()
            arT = b16((128, 4, 2, 128))
            bkT = b16((128, 4, 2, 128))
            tmo = b16((128, 12, 128))
            Am = [b16((128, 4, 128)) for _ in range(8)]
            X = [[b16() for _ in range(2)] for _ in range(2)]
            XT = [[b16() for _ in range(2)] for _ in range(2)]
            Pm = [b16() for _ in range(2)]
            W0t = [b16((128, 128)) for _ in range(4)]
            Ut = [b16((128, 128)) for _ in range(4)]
            id4 = b16()
            mL4 = f32()
            for q in range(4):
                self.op(PL, lambda e: e.tensor_copy(out=id4.ap[:, q, :], in_=self.identb.ap[:]), (self.identb,), (id4,))
                self.op(PL, lambda e: e.tensor_copy(out=mL4.ap[:, q, :], in_=self.maskL.ap[:]), (self.maskL,), (mL4,))
                self.op(PL, lambda e: e.memset(W0t[q].ap[:], 0.0), (), (W0t[q],))
                self.op(PL, lambda e: e.memset(Ut[q].ap[:], 0.0), (), (Ut[q],))
            self.op(PL, lambda e: e.memset(hl.ap[:], 0.0), (), (hl,))
            fl = lambda bf: bf.ap.rearrange("p a b -> p (a b)")
            for tc in range(4):
                xt = self.x_tm[tc]
                self.op(DV, lambda e: e.tensor_copy(out=xT32.ap[:, :, 0:1], in_=self.xlast[j].ap[:].unsqueeze(2)),
                        (self.xlast[j],), (xT32,))
                for k4 in range(2):
                    b = self.bank()
                    for q in range(4):
                        kc = k4 * 4 + q
                        self.op(PE, lambda e: e.transpose(b.ap[:, q * 128:(q + 1) * 128], xt.ap[:, kc * 128:(kc + 1) * 128],
                                                          self.ident32.ap[:]), (xt, self.ident32), (b,), inc=(q == 3))
                    self.copy(self.evq(), xT32.ap[:, k4 * 4:(k4 + 1) * 4, 1:129],
                              b.ap.rearrange("p (a b) -> p a b", b=128), (b,), (xT32,))
                self.op(DV, lambda e: e.tensor_copy(out=self.xlast[j].ap[:].unsqueeze(2), in_=xT32.ap[:, :, 128:129]),
                        (xT32,), (self.xlast[j],))
                self.op(DV, lambda e: e.tensor_tensor(out=xx.ap[:], in0=xT32.ap[:, :, 0:128], in1=xT32.ap[:, :, 1:129],
                                                      op=ALU.subtract), (xT32,), (xx,))
                for m in range(6):
                    for kc in range(8):
                        self.op(DV, lambda e: e.scalar_tensor_tensor(out=xm[m].ap[:, kc, :], in0=xx.ap[:, kc, :],
                                                                     scalar=pc(kc, P0 + m), in1=xT32.ap[:, kc, 1:129],
                                                                     op0=ALU.mult, op1=ALU.add), (xx, xT32, pf), (xm[m],))
                b = self.bank()
                b2 = self.bank()
                for (bb_, r_, c_, l0, lw, mi) in ((b, 64, 0, 0, 64, 3), (b, 64, 128, 64, 64, 4), (b, 32, 256, 128, 32, 2),
                                                 (b2, 128, 0, 160, 128, 5), (b2, 32, 128, 288, 32, 5)):
                    for kc in range(8):
                        self.mm(bb_.ap[0:r_, c_:c_ + 128], L1.ap[:, kc, l0:l0 + lw], xm[mi].ap[:, kc, :], kc == 0, kc == 7,
                                (L1, xm[mi]), (bb_,))
                self.op(AC, lambda e: e.activation(out=hl.ap[0:64, 0, :], in_=b.ap[0:64, 0:128], func=AF.Tanh), (b,), (hl,))
                self.op(AC, lambda e: e.activation(out=hl.ap[0:64, 1, :], in_=b.ap[0:64, 128:256], func=AF.Copy), (b,), (hl,))
                self.op(AC, lambda e: e.activation(out=hl.ap[0:32, 2, :], in_=b.ap[0:32, 256:384], func=AF.Copy), (b,), (hl,))
                self.op(AC, lambda e: e.activation(out=hl.ap[:, 3, :], in_=b2.ap[:, 0:128], func=AF.Sigmoid), (b2,), (hl,))
                self.op(AC, lambda e: e.activation(out=hl.ap[0:32, 4, :], in_=b2.ap[0:32, 128:256], func=AF.Sigmoid), (b2,), (hl,))
                for hq in range(2):
                    H0 = hq * 4
                    for (mi, dst) in ((0, r32), (1, k32), (2, v32)):
                        s = self.load_kc_block(S["rwkv_w_rkv"][j, mi], hq * 512, 512)
                        b = self.bank()
                        for q in range(4):
                            for kc in range(8):
                                self.mm(b.ap[:, q * 128:(q + 1) * 128], s.ap[:, kc, q * 128:(q + 1) * 128],
                                        xm[mi].ap[:, kc, :], kc == 0, kc == 7, (s, xm[mi]), (b,))
                        self.copy(self.evq(), fl(dst), b.ap[:], (b,), (dst,))
                    bw, ba, bg = self.bank(), self.bank(), self.bank()
                    for q in range(4):
                        cs = slice((H0 + q) * 128, (H0 + q + 1) * 128)
                        qs = slice(q * 128, (q + 1) * 128)
                        self.mm(bw.ap[:, qs], L2[0].ap[:, cs], hl.ap[0:64, 0, :], True, True, (L2[0], hl), (bw,))
                        self.mm(ba.ap[:, qs], L2[1].ap[:, cs], hl.ap[0:64, 1, :], True, True, (L2[1], hl), (ba,))
                        self.mm(bg.ap[:, qs], L2[3].ap[:, cs], hl.ap[:, 3, :], True, False, (L2[3], hl), (bg,))
                        self.mm(bg.ap[:, qs], L2[4].ap[:, cs], hl.ap[0:32, 4, :], False, True, (L2[4], hl), (bg,))
                    for q in range(4):
                        qs = slice(q * 128, (q + 1) * 128)
                        self.op(AC, lambda e: e.activation(out=lds.ap[:, q, :], in_=bw.ap[:, qs], func=AF.Sigmoid,
                                                           bias=pc(H0 + q, P0 + 6), scale=1.0), (bw, pf), (lds,))
                        self.op(AC, lambda e: e.activation(out=a32.ap[:, q, :], in_=ba.ap[:, qs], func=AF.Sigmoid,
                                                           bias=pc(H0 + q, P0 + 7), scale=1.0), (ba, pf), (a32,))
                    self.copy(DV, fl(g32), bg.ap[:], (bg,), (g32,))
                    self.op(DV, lambda e: e.tensor_tensor_scan(out=fl(Lc), data0=self.rmask.ap[:, 0:512], data1=fl(lds),
                                                               initial=0.0, op0=ALU.mult, op1=ALU.add),
                            (lds, self.rmask), (Lc,))
                    self.op(AC, lambda e: e.activation(out=fl(G), in_=fl(Lc), func=AF.Exp, scale=-CDEC), (Lc,), (G,))
                    self.op(AC, lambda e: e.activation(out=fl(Gi), in_=fl(Lc), func=AF.Exp, scale=CDEC), (Lc,), (Gi,))
                    self.op(DV, lambda e: e.tensor_tensor(out=fl(lds), in0=fl(Lc), in1=fl(lds), op=ALU.subtract),
                            (Lc, lds), (lds,))
                    self.op(AC, lambda e: e.activation(out=fl(Gp), in_=fl(lds), func=AF.Exp, scale=-CDEC), (lds,), (Gp,))
                    kkp = Lc
                    for q in range(4):
                        self.op(DV, lambda e: e.tensor_scalar(out=kkp.ap[:, q, :], in0=k32.ap[:, q, :],
                                                              scalar1=pc(H0 + q, P0 + 8), scalar2=None, op0=ALU.mult),
                                (k32, pf, G, Gi), (kkp,))
                    self.op(PL, lambda e: e.tensor_tensor(out=fl(sqb), in0=fl(kkp), in1=fl(kkp), op=ALU.mult), (kkp,), (sqb,))
                    b = self.bank()
                    for q in range(4):
                        self.mm(b.ap[:, q * 128:(q + 1) * 128], self.bones.ap[:], sqb.ap[:, q, :], True, True,
                                (self.bones, sqb), (b,))
                    self.op(AC, lambda e: e.activation(out=fl(T1), in_=b.ap[:], func=AF.Sqrt, bias=self.epsln.ap[:, 2:3],
                                                       scale=1.0), (b, self.epsln), (T1,))
                    self.op(DV, lambda e: e.reciprocal(out=fl(T1), in_=fl(T1)), (T1,), (T1,))
                    kk = kkp
                    self.op(DV, lambda e: e.tensor_tensor(out=fl(kk), in0=fl(kkp), in1=fl(T1), op=ALU.mult), (kkp, T1), (kk,))
                    for q in range(4):
                        self.op(DV, lambda e: e.tensor_scalar(out=T1.ap[:, q, :], in0=a32.ap[:, q, :],
                                                              scalar1=pc(H0 + q, P0 + 9), scalar2=pc(H0 + q, 32 + j),
                                                              op0=ALU.mult, op1=ALU.add), (a32, pf), (T1,))
                    km = T1
                    self.op(PL, lambda e: e.tensor_tensor(out=fl(km), in0=fl(T1), in1=fl(k32), op=ALU.mult), (T1, k32), (km,))
                    bbb = a32
                    self.op(PL, lambda e: e.tensor_tensor(out=fl(bbb), in0=fl(kk), in1=fl(a32), op=ALU.mult), (kk, a32), (bbb,))
                    self.op(DV, lambda e: e.tensor_tensor(out=arT.ap[:, :, 1, :], in0=r32.ap[:], in1=G.ap[:], op=ALU.mult),
                            (r32, G), (arT,))
                    self.op(DV, lambda e: e.scalar_tensor_tensor(out=arT.ap[:, :, 0, :], in0=kk.ap[:], scalar=-1.0,
                                                                 in1=Gp.ap[:], op0=ALU.mult, op1=ALU.mult), (kk, Gp), (arT,))
                    self.op(PL, lambda e: e.tensor_tensor(out=bkT.ap[:, :, 0, :], in0=bbb.ap[:], in1=Gi.ap[:], op=ALU.mult),
                            (bbb, Gi), (bkT,))
                    self.op(PL, lambda e: e.tensor_tensor(out=bkT.ap[:, :, 1, :], in0=km.ap[:], in1=Gi.ap[:], op=ALU.mult),
                            (km, Gi), (bkT,))
                    vf_ = self.vfirst.ap[:, tc, H0:H0 + 4, :]
                    if j == 0:
                        self.op(PL, lambda e: e.tensor_copy(out=vf_, in_=v32.ap[:]), (v32,), (self.vfirst,))
                    else:
                        bz = self.bank()
                        for q in range(4):
                            cs = slice((H0 + q) * 128, (H0 + q + 1) * 128)
                            self.mm(bz.ap[:, q * 128:(q + 1) * 128], L2[2].ap[:, cs], hl.ap[0:32, 2, :], True, True,
                                    (L2[2], hl), (bz,))
                        sv = Gp
                        for q in range(4):
                            self.op(AC, lambda e: e.activation(out=sv.ap[:, q, :], in_=bz.ap[:, q * 128:(q + 1) * 128],
                                                               func=AF.Sigmoid, bias=pc(H0 + q, P0 + 13), scale=1.0),
                                    (bz, pf, arT), (sv,))
                        dd = Gi
                        self.op(PL, lambda e: e.tensor_tensor(out=dd.ap[:], in0=vf_, in1=v32.ap[:], op=ALU.subtract),
                                (self.vfirst, v32, bkT), (dd,))
                        self.op(PL, lambda e: e.tensor_tensor(out=dd.ap[:], in0=dd.ap[:], in1=sv.ap[:], op=ALU.mult),
                                (dd, sv), (dd,))
                        self.op(PL, lambda e: e.tensor_tensor(out=v32.ap[:], in0=v32.ap[:], in1=dd.ap[:], op=ALU.add),
                                (v32, dd), (v32,))
                    self.op(PL, lambda e: e.tensor_copy(out=vb.ap[:], in_=v32.ap[:]), (v32,), (vb,))
                    for q in range(4):
                        self.op(DV, lambda e: e.scalar_tensor_tensor(out=rkb.ap[:, q, :], in0=r32.ap[:, q, :],
                                                                     scalar=pc(H0 + q, P0 + 10), in1=km.ap[:, q, :],
                                                                     op0=ALU.mult, op1=ALU.mult), (r32, pf, km), (rkb,))
                    b = self.bank()
                    for q in range(4):
                        self.mm(b.ap[:, q * 128:(q + 1) * 128], self.bones.ap[:], rkb.ap[:, q, :], True, True,
                                (self.bones, rkb), (b,))
                    self.op(DV, lambda e: e.tensor_tensor(out=fl(bon), in0=b.ap[:], in1=fl(v32), op=ALU.mult), (b, v32), (bon,))
                    for q in range(4):
                        if q % 4 == 0:
                            pass
                    srcs = []
                    for q in range(4):
                        srcs += [bkT.ap[:, q, 0, :], bkT.ap[:, q, 1, :], vb.ap[:, q, :]]
                    for bi in range(3):
                        b = self.bank()
                        for si in range(4):
                            self.mm(b.ap[:, si * 128:(si + 1) * 128], srcs[bi * 4 + si], self.identb.ap[:], True, True,
                                    (bkT, vb, self.identb), (b,))
                        self.copy(self.evq(), tmo.ap[:, bi * 4:(bi + 1) * 4, :].rearrange("p a b -> p (a b)"), b.ap[:],
                                  (b,), (tmo,))
                    for q in range(4):
                        for h in range(2):
                            hs = slice(h * 64, (h + 1) * 64)
                            u = q * 2 + h
                            b = self.bank()
                            rhs = arT.ap[hs, q, :, :].rearrange("p a b -> p (a b)")
                            self.mm(b.ap[:, 0:256], bkT.ap[hs, q, 0, :], rhs, True, True, (bkT, arT), (b,))
                            self.mm(b.ap[:, 256:512], bkT.ap[hs, q, 1, :], rhs, True, True, (bkT, arT), (b,))
                            self.op(DV, lambda e: e.tensor_tensor(out=fl(Am[u]), in0=b.ap[:],
                                                                  in1=self.mask4.ap.rearrange("p a b c -> p (a b c)"),
                                                                  op=ALU.mult), (b, self.mask4), (Am[u],))
                    for h in range(2):
                        hs = slice(h * 64, (h + 1) * 64)
                        b = self.bank()
                        for q in range(4):
                            self.mm(b.ap[:, q * 128:(q + 1) * 128], arT.ap[hs, q, 0, :], bkT.ap[hs, q, 0, :], True, True,
                                    (arT, bkT), (b,))
                        self.op(DV, lambda e: e.tensor_tensor(out=fl(XT[h][0]), in0=b.ap[:], in1=fl(mL4), op=ALU.mult),
                                (b, mL4), (XT[h][0],))
                        for q in range(4):
                            self.op(PL, lambda e: e.tensor_copy(out=X[h][0].ap[:, q, :], in_=Am[q * 2 + h].ap[:, 0, :]),
                                    (Am[q * 2 + h],), (X[h][0],))
                        self.op(PL, lambda e: e.tensor_tensor(out=fl(Pm[h]), in0=fl(X[h][0]), in1=fl(id4), op=ALU.add),
                                (X[h][0], id4), (Pm[h],))
                    for kst in range(5):
                        cur, nxt = kst % 2, (kst + 1) % 2
                        for h in range(2):
                            if kst < 4:
                                bA = self.bank()
                                for q in range(4):
                                    self.mm(bA.ap[:, q * 128:(q + 1) * 128], XT[h][cur].ap[:, q, :], X[h][cur].ap[:, q, :],
                                            True, True, (XT[h][cur], X[h][cur]), (bA,))
                            bB = self.bank()
                            for q in range(4):
                                self.mm(bB.ap[:, q * 128:(q + 1) * 128], X[h][cur].ap[:, q, :], XT[h][cur].ap[:, q, :],
                                        True, True, (XT[h][cur], X[h][cur]), (bB,))
                            if kst < 4:
                                self.copy(AC, fl(X[h][nxt]), bA.ap[:], (bA,), (X[h][nxt],))
                            self.copy(DV, fl(XT[h][nxt]), bB.ap[:], (bB,), (XT[h][nxt],))
                        for h in range(2):
                            bC = self.bank()
                            for q in range(4):
                                self.mm(bC.ap[:, q * 128:(q + 1) * 128], XT[h][nxt].ap[:, q, :], Pm[h].ap[:, q, :],
                                        True, True, (XT[h][nxt], Pm[h]), (bC,))
                            self.op(DV, lambda e: e.tensor_tensor(out=fl(Pm[h]), in0=bC.ap[:], in1=fl(Pm[h]), op=ALU.add),
                                    (bC, Pm[h]), (Pm[h],))
                    for c2 in range(2):
                        cs = slice(c2 * 64, (c2 + 1) * 64)
                        bks = []
                        for q in range(4):
                            hp = H0 + q
                            Sb = self.Sb[j][hp]
                            b = self.bank()
                            bks.append(b)
                            self.mm(b.ap[cs, 0:128], arT.ap[:, q, 0, cs], Sb.ap[:], True, False, (arT, Sb), (b,))
                            for h in range(2):
                                self.mm(b.ap[cs, h * 64:(h + 1) * 64], Am[q * 2 + h].ap[:, 2, cs],
                                        tmo.ap[:, q * 3 + 2, h * 64:(h + 1) * 64], False, h == 1, (Am[q * 2 + h], tmo), (b,))
                        for q in range(4):
                            self.copy(self.evq(), W0t[q].ap[cs, :], bks[q].ap[cs, 0:128], (bks[q],), (W0t[q],))
                        bks = []
                        for q in range(4):
                            b = self.bank()
                            bks.append(b)
                            for h in range(2):
                                self.mm(b.ap[cs, h * 64:(h + 1) * 64], Pm[h].ap[:, q, cs], W0t[q].ap[:, h * 64:(h + 1) * 64],
                                        True, True, (Pm[h], W0t[q]), (b,))
                        for q in range(4):
                            self.copy(self.evq(), Ut[q].ap[cs, :], bks[q].ap[cs, 0:128], (bks[q],), (Ut[q],))
                        for q in range(4):
                            hp = H0 + q
                            Sb = self.Sb[j][hp]
                            S32 = self.S32[j][hp]
                            by = self.bank()
                            self.mm(by.ap[:, 0:64], Sb.ap[:], arT.ap[:, q, 1, cs], True, False, (Sb, arT), (by,))
                            for h in range(2):
                                hs = slice(h * 64, (h + 1) * 64)
                                self.mm(by.ap[hs, 0:64], Ut[q].ap[:, hs], Am[q * 2 + h].ap[:, 1, cs], False, False,
                                        (Ut[q], Am[q * 2 + h]), (by,))
                                self.mm(by.ap[hs, 0:64], tmo.ap[:, q * 3 + 2, hs], Am[q * 2 + h].ap[:, 3, cs], False, h == 1,
                                        (tmo, Am[q * 2 + h]), (by,))
                            self.copy(AC, y32.ap[:, q, cs], by.ap[:, 0:64], (by,), (y32,))
                            bs = self.bank()
                            self.mm(bs.ap[:, 0:128], tmo.ap[cs, q * 3 + 0, :], Ut[q].ap[cs, :], True, False, (tmo, Ut[q]), (bs,))
                            self.mm(bs.ap[:, 0:128], tmo.ap[cs, q * 3 + 1, :], tmo.ap[cs, q * 3 + 2, :], False, True,
                                    (tmo,), (bs,))
                            gc = G.ap[:, q, c2 * 64 + 63:c2 * 64 + 64]
                            self.op(DV, lambda e: e.tensor_scalar(out=S32.ap[:], in0=S32.ap[:], scalar1=gc, scalar2=None,
                                                                  op0=ALU.mult), (S32, G), (S32,))
                            for h in range(2):
                                hs = slice(h * 64, (h + 1) * 64)
                                self.op(DV, lambda e: e.scalar_tensor_tensor(out=S32.ap[hs, hs], in0=bs.ap[hs, hs],
                                                                             scalar=G.ap[hs, q, c2 * 64 + 63:c2 * 64 + 64],
                                                                             in1=S32.ap[hs, hs], op0=ALU.mult, op1=ALU.add),
                                        (bs, G, S32), (S32,))
                            self.copy(AC, Sb.ap[:], S32.ap[:], (S32,), (Sb,))
                    self.op(PL, lambda e: e.tensor_copy(out=yb.ap[:], in_=y32.ap[:]), (y32,), (yb,))
                    self.op(PL, lambda e: e.tensor_tensor(out=sqb.ap[:], in0=y32.ap[:], in1=y32.ap[:], op=ALU.mult), (y32,), (sqb,))
                    b1, b2_ = self.bank(), self.bank()
                    for q in range(4):
                        self.mm(b1.ap[:, q * 128:(q + 1) * 128], self.bones.ap[:], yb.ap[:, q, :], True, True, (self.bones, yb), (b1,))
                        self.mm(b2_.ap[:, q * 128:(q + 1) * 128], self.bones.ap[:], sqb.ap[:, q, :], True, True, (self.bones, sqb), (b2_,))
                    mean, var = r32, k32
                    self.op(AC, lambda e: e.activation(out=fl(mean), in_=b1.ap[:], func=AF.Copy, scale=1.0 / 64), (b1, arT, bkT), (mean,))
                    self.op(PL, lambda e: e.tensor_tensor(out=fl(var), in0=fl(mean), in1=fl(mean), op=ALU.mult), (mean, bkT), (var,))
                    self.op(DV, lambda e: e.scalar_tensor_tensor(out=fl(var), in0=b2_.ap[:], scalar=1.0 / 64, in1=fl(var),
                                                                 op0=ALU.mult, op1=ALU.subtract), (b2_, var), (var,))
                    self.op(AC, lambda e: e.activation(out=fl(var), in_=fl(var), func=AF.Sqrt, bias=self.epsln.ap[:, 3:4],
                                                       scale=1.0), (var, self.epsln), (var,))
                    self.op(DV, lambda e: e.reciprocal(out=fl(var), in_=fl(var)), (var,), (var,))
                    self.op(DV, lambda e: e.tensor_tensor(out=fl(y32), in0=fl(y32), in1=fl(mean), op=ALU.subtract), (y32, mean), (y32,))
                    self.op(DV, lambda e: e.tensor_tensor(out=fl(y32), in0=fl(y32), in1=fl(var), op=ALU.mult), (y32, var), (y32,))
                    for q in range(4):
                        self.op(DV, lambda e: e.tensor_scalar(out=y32.ap[:, q, :], in0=y32.ap[:, q, :],
                                                              scalar1=pc(H0 + q, P0 + 11), scalar2=pc(H0 + q, P0 + 12),
                                                              op0=ALU.mult, op1=ALU.add), (y32, pf), (y32,))
                    self.op(PL, lambda e: e.tensor_tensor(out=fl(y32), in0=fl(y32), in1=fl(bon), op=ALU.add), (y32, bon), (y32,))
                    self.op(PL, lambda e: e.tensor_tensor(out=yg.ap[:, H0:H0 + 4, :], in0=y32.ap[:], in1=g32.ap[:], op=ALU.mult),
                            (y32, g32), (yg,))
                bk = [self.bank(), self.bank()]
                for hb in range(2):
                    s, v = self.load_rows_block(S["rwkv_w_o"][j], hb * 512, 4)
                    for q in range(4):
                        hp = hb * 4 + q
                        for dh in range(2):
                            self.mm(bk[dh].ap[:], yg.ap[:, hp, :], v[:, q, dh * 512:(dh + 1) * 512], hp == 0, hp == 7,
                                    (yg, s), (bk[dh],), inc=True)
                self.resid_ln(tc, bk[0], bk[1], 1.0 / ALPHA, lnb)


N_TILES = SEQ // TILE
N_SUB = 12


def kernel(**inputs):
    kb = KB(N_TILES, N_SUB)
    nc = kb.build()
    x = np.ascontiguousarray(inputs["x"], dtype=np.float32)
    B = x.shape[0]
    w = {k: np.ascontiguousarray(inputs[k], dtype=np.float32) for k in SHAPES}
    in_maps = []
    for b in range(B):
        m = dict(w)
        m["x"] = x[b, :N_TILES * TILE]
        in_maps.append(m)
    res = run_bass_kernel_spmd(nc, in_maps, core_ids=list(range(B)))
    return np.stack([res.results[b]["y"] for b in range(B)], axis=0)
```
